# Optimizing a Trainium2 kernel written in Bass

```python
import math
import jax, jax.numpy as jnp
from jax import lax
import numpy as np

D_MODEL = 2048
BATCH = 2
SEQ = 4096
DEPTH = 1
DEC_BATCH = 32
DEC_SEQ = 8
PAST_LEN = 8192
PAGE_SIZE = 128

N_RET_HEADS = 8
RET_DK = 128
RET_DV = 128
RET_CHUNK = 128
ROPE_BASE = 10000.0
N_DIFF_HEADS = 8
DIFF_D = 64
DIFF_DV = 2 * DIFF_D
Q_BLOCK = 128
N_BUCKETS = 32
MAX_EXACT = N_BUCKETS // 2
MAX_DISTANCE = 128
N_MEM = 256
N_XHEADS = 4
X_HEAD_DIM = 128
D_FF = -(-8 * D_MODEL // (3 * 256)) * 256
RMS_EPS = 1e-6
RET_W = N_RET_HEADS * RET_DV
DIFF_W = N_DIFF_HEADS * DIFF_DV
MIX_WIDTH = RET_W + DIFF_W
IN_COLS = (N_RET_HEADS * RET_DK, N_RET_HEADS * RET_DK, RET_W, RET_W,
           N_DIFF_HEADS * 2 * DIFF_D, N_DIFF_HEADS * 2 * DIFF_D, DIFF_W)
D_IN = sum(IN_COLS)
IN_SPLITS = tuple(int(v) for v in np.cumsum(IN_COLS)[:-1])

kernel_name = 'hymba_retention_diffattn_decoder_step'


def rms_norm(x, g):
    xf = x.astype(jnp.float32)
    y = xf * lax.rsqrt(jnp.mean(xf * xf, axis=-1, keepdims=True) + RMS_EPS)
    return (y * g.astype(jnp.float32)).astype(x.dtype)


def head_rms_norm(x, g):
    h, d = x.shape[-2], x.shape[-1]
    xf = x.astype(jnp.float32)
    y = xf * lax.rsqrt(jnp.mean(xf * xf, axis=-1, keepdims=True) + RMS_EPS)
    return y * g.astype(jnp.float32).reshape(h, d)


def rotate(x, pos):
    d = x.shape[-1]
    inv_freq = ROPE_BASE ** (-jnp.arange(0, d, 2, dtype=jnp.float32) / d)
    ang = pos.astype(jnp.float32)[:, None] * inv_freq[None, :]
    cos = jnp.cos(ang)[None, :, None, :]
    sin = jnp.sin(ang)[None, :, None, :]
    xf = x.astype(jnp.float32)
    x1, x2 = xf[..., 0::2], xf[..., 1::2]
    return jnp.stack([x1 * cos - x2 * sin, x1 * sin + x2 * cos], axis=-1).reshape(x.shape)


def retention_chunked(q, k, v, s0):
    b, t, h, dk = q.shape
    dv = v.shape[-1]
    c = math.gcd(t, RET_CHUNK)
    n = t // c
    log_g = jnp.log1p(-(2.0 ** (-5.0 - jnp.arange(h, dtype=jnp.float32))))
    idx = jnp.arange(c, dtype=jnp.float32)
    rel = idx[:, None] - idx[None, :]
    intra = jnp.where(rel[None] >= 0, jnp.exp(log_g[:, None, None] * jnp.maximum(rel, 0.0)[None]), 0.0)
    q_dec = jnp.exp(log_g[None, :] * (idx[:, None] + 1.0))[None, :, :, None]
    k_dec = jnp.exp(log_g[None, :] * (c - 1.0 - idx[:, None]))[None, :, :, None]
    chunk_dec = jnp.exp(log_g * c)[None, :, None, None]

    def to_chunks(a):
        return jnp.moveaxis(a.astype(jnp.float32).reshape(b, n, c, h, a.shape[-1]), 1, 0)

    def step(s, inp):
        qb, kb, vb = inp
        scores = jnp.einsum('bihd,bjhd->bhij', qb, kb) * intra[None]
        o = jnp.einsum('bhij,bjhv->bihv', scores, vb) + jnp.einsum('bihd,bhdv->bihv', qb, s) * q_dec
        s_new = s * chunk_dec + jnp.einsum('bjhd,bjhv->bhdv', kb * k_dec, vb)
        return s_new, o

    s_fin, o = lax.scan(step, s0.astype(jnp.float32), (to_chunks(q), to_chunks(k), to_chunks(v)))
    return jnp.moveaxis(o, 0, 1).reshape(b, t, h, dv), s_fin


def t5_bucket(qpos, kpos):
    n = jnp.maximum(qpos[:, None] - kpos[None, :], 0)
    nf = jnp.maximum(n, 1).astype(jnp.float32)
    large = MAX_EXACT + (jnp.log(nf / MAX_EXACT) / math.log(MAX_DISTANCE / MAX_EXACT)
                         * (N_BUCKETS - MAX_EXACT)).astype(jnp.int32)
    large = jnp.minimum(large, N_BUCKETS - 1)
    return jnp.where(n < MAX_EXACT, n, large)


def diff_attend(q, k, v, qpos, kpos, lam, rel_bias):
    scale = DIFF_D ** -0.5
    bias = jnp.transpose(rel_bias.astype(jnp.float32)[t5_bucket(qpos, kpos)], (2, 0, 1))[None]
    causal = (kpos[None, :] <= qpos[:, None])[None, None]

    def softmax_map(qa, ka):
        s = jnp.einsum('bqhd,bkhd->bhqk', qa, ka, preferred_element_type=jnp.float32) * scale + bias
        return jax.nn.softmax(jnp.where(causal, s, -1e30), axis=-1)

    a = softmax_map(q[..., :DIFF_D], k[..., :DIFF_D]) - lam * softmax_map(q[..., DIFF_D:], k[..., DIFF_D:])
    return jnp.einsum('bhqk,bkhv->bqhv', a, v.astype(jnp.float32))


def cross_attend(h, mem_k, mem_v, w_xq, w_xo):
    b, t, _ = h.shape
    q = (h @ w_xq).reshape(b, t, N_XHEADS, X_HEAD_DIM)
    s = jnp.einsum('bqhd,bkhd->bhqk', q, mem_k, preferred_element_type=jnp.float32) * X_HEAD_DIM ** -0.5
    p = jax.nn.softmax(s, axis=-1)
    o = jnp.einsum('bhqk,bkhd->bqhd', p, mem_v.astype(jnp.float32)).astype(h.dtype)
    return o.reshape(b, t, N_XHEADS * X_HEAD_DIM) @ w_xo


def decoder_layer(x, pos, ret_s0, k_past, v_past, mem_k, mem_v, rel_bias, lam_init,
                  n_pre_mix, n_post_mix, n_pre_x, n_post_x, n_pre_ffn, n_post_ffn,
                  w_in, w_out, ret_gn, diff_gn, lam_q1, lam_k1, lam_q2, lam_k2,
                  w_xq, w_xo, w_gate, w_up, w_down):
    b, t, _ = x.shape
    h = rms_norm(x, n_pre_mix)
    rq, rk, rv, rg, dq, dk, dv = jnp.split(h @ w_in, IN_SPLITS, axis=-1)
    rq = rotate(rq.reshape(b, t, N_RET_HEADS, RET_DK), pos)
    rk = rotate(rk.reshape(b, t, N_RET_HEADS, RET_DK), pos) * RET_DK ** -0.5
    rv = rv.reshape(b, t, N_RET_HEADS, RET_DV)
    ret_o, ret_s = retention_chunked(rq, rk, rv, ret_s0)
    ret_o = head_rms_norm(ret_o, ret_gn).reshape(b, t, RET_W).astype(x.dtype)
    ret_o = jax.nn.silu(rg) * ret_o
    dq = dq.reshape(b, t, N_DIFF_HEADS, 2 * DIFF_D)
    dk = dk.reshape(b, t, N_DIFF_HEADS, 2 * DIFF_D)
    dv = dv.reshape(b, t, N_DIFF_HEADS, DIFF_DV)
    lam = (jnp.exp(jnp.sum(lam_q1.astype(jnp.float32) * lam_k1.astype(jnp.float32)))
           - jnp.exp(jnp.sum(lam_q2.astype(jnp.float32) * lam_k2.astype(jnp.float32))) + lam_init)
    if k_past is None:
        qb_len = math.gcd(t, Q_BLOCK)
        nb = t // qb_len
        q_blocks = jnp.moveaxis(dq.reshape(b, nb, qb_len, N_DIFF_HEADS, 2 * DIFF_D), 1, 0)
        pos_blocks = pos.reshape(nb, qb_len)
        o_blocks = lax.map(lambda a: diff_attend(a[0], dk, dv, a[1], pos, lam, rel_bias), (q_blocks, pos_blocks))
        diff_o = jnp.moveaxis(o_blocks, 0, 1).reshape(b, t, N_DIFF_HEADS, DIFF_DV)
    else:
        past = k_past.shape[1]
        keys = jnp.concatenate([k_past, dk], axis=1)
        vals = jnp.concatenate([v_past, dv], axis=1)
        diff_o = diff_attend(dq, keys, vals, pos, jnp.arange(past + t), lam, rel_bias)
    diff_o = (head_rms_norm(diff_o, diff_gn) * (1.0 - lam_init)).reshape(b, t, DIFF_W).astype(x.dtype)
    mix = jnp.concatenate([ret_o, diff_o], axis=-1) @ w_out
    x = x + rms_norm(mix, n_post_mix)
    x = x + rms_norm(cross_attend(rms_norm(x, n_pre_x), mem_k, mem_v, w_xq, w_xo), n_post_x)
    h = rms_norm(x, n_pre_ffn)
    f = (jax.nn.silu(h @ w_gate) * (h @ w_up)) @ w_down
    x = x + rms_norm(f, n_post_ffn)
    return x, dk, dv, ret_s


def setup_inputs(seed: int = 0) -> dict:
    key = jax.random.key(seed)
    ks = iter(jax.random.split(key, 48))
    n_pages = PAST_LEN // PAGE_SIZE
    n_used = DEC_BATCH * n_pages
    n_phys = (n_used * 5 + 3) // 4

    def nrm(shape, scale):
        return jax.random.normal(next(ks), shape, jnp.float32) * scale

    def gain():
        return 1.0 + nrm((DEPTH, D_MODEL), 0.05)

    page_table = jax.random.permutation(next(ks), n_phys)[:n_used].reshape(DEC_BATCH, n_pages).astype(jnp.int32)
    return {
        'x_prompt': nrm((BATCH, SEQ, D_MODEL), 1.0),
        'x_sample': nrm((DEC_BATCH, DEC_SEQ, D_MODEL), 1.0),
        'cache_k': nrm((DEPTH, n_phys, PAGE_SIZE, N_DIFF_HEADS, 2 * DIFF_D), 1.0),
        'cache_v': nrm((DEPTH, n_phys, PAGE_SIZE, N_DIFF_HEADS, DIFF_DV), 1.0),
        'state_ret': nrm((DEPTH, DEC_BATCH, N_RET_HEADS, RET_DK, RET_DV), 1.0),
        'cache_mem_k': nrm((DEPTH, DEC_BATCH, N_MEM, N_XHEADS, X_HEAD_DIM), 1.0),
        'cache_mem_v': nrm((DEPTH, DEC_BATCH, N_MEM, N_XHEADS, X_HEAD_DIM), 1.0),
        'page_table': page_table,
        'mem_prompt': nrm((BATCH, N_MEM, D_MODEL), 1.0),
        'rel_bias': nrm((N_BUCKETS, N_DIFF_HEADS), 0.5),
        'norm_pre_mix': gain(),
        'norm_post_mix': gain(),
        'norm_pre_x': gain(),
        'norm_post_x': gain(),
        'norm_pre_ffn': gain(),
        'norm_post_ffn': gain(),
        'norm_mem': gain(),
        'w_in': nrm((DEPTH, D_MODEL, D_IN), D_MODEL ** -0.5),
        'w_out': nrm((DEPTH, MIX_WIDTH, D_MODEL), MIX_WIDTH ** -0.5),
        'ret_gn': 1.0 + nrm((DEPTH, RET_W), 0.05),
        'diff_gn': 1.0 + nrm((DEPTH, DIFF_W), 0.05),
        'lam_q1': nrm((DEPTH, DIFF_D), 0.1),
        'lam_k1': nrm((DEPTH, DIFF_D), 0.1),
        'lam_q2': nrm((DEPTH, DIFF_D), 0.1),
        'lam_k2': nrm((DEPTH, DIFF_D), 0.1),
        'w_xq': nrm((DEPTH, D_MODEL, N_XHEADS * X_HEAD_DIM), D_MODEL ** -0.5),
        'w_xk': nrm((DEPTH, D_MODEL, N_XHEADS * X_HEAD_DIM), D_MODEL ** -0.5),
        'w_xv': nrm((DEPTH, D_MODEL, N_XHEADS * X_HEAD_DIM), D_MODEL ** -0.5),
        'w_xo': nrm((DEPTH, N_XHEADS * X_HEAD_DIM, D_MODEL), (N_XHEADS * X_HEAD_DIM) ** -0.5),
        'w_gate': nrm((DEPTH, D_MODEL, D_FF), D_MODEL ** -0.5),
        'w_up': nrm((DEPTH, D_MODEL, D_FF), D_MODEL ** -0.5),
        'w_down': nrm((DEPTH, D_FF, D_MODEL), D_FF ** -0.5),
    }


def reference(x_prompt, x_sample, cache_k, cache_v, state_ret, cache_mem_k, cache_mem_v, page_table, mem_prompt,
              rel_bias, norm_pre_mix, norm_post_mix, norm_pre_x, norm_post_x, norm_pre_ffn, norm_post_ffn, norm_mem,
              w_in, w_out, ret_gn, diff_gn, lam_q1, lam_k1, lam_q2, lam_k2,
              w_xq, w_xk, w_xv, w_xo, w_gate, w_up, w_down):
    b_p, t_p, _ = x_prompt.shape
    b_s, t_s, _ = x_sample.shape
    n_mem = mem_prompt.shape[1]
    past_len = page_table.shape[1] * cache_k.shape[2]
    pos_p = jnp.arange(t_p)
    pos_s = past_len + jnp.arange(t_s)
    xp, xs = x_prompt, x_sample
    kp_l, vp_l, sp_l, mkp_l, mvp_l, ks_l, vs_l, ss_l = [], [], [], [], [], [], [], []
    for l in range(DEPTH):
        lam_init = 0.8 - 0.6 * math.exp(-0.3 * l)
        shared = (rel_bias, lam_init, norm_pre_mix[l], norm_post_mix[l], norm_pre_x[l], norm_post_x[l],
                  norm_pre_ffn[l], norm_post_ffn[l], w_in[l], w_out[l], ret_gn[l], diff_gn[l],
                  lam_q1[l], lam_k1[l], lam_q2[l], lam_k2[l], w_xq[l], w_xo[l], w_gate[l], w_up[l], w_down[l])
        mem_n = rms_norm(mem_prompt, norm_mem[l])
        mk_p = (mem_n @ w_xk[l]).reshape(b_p, n_mem, N_XHEADS, X_HEAD_DIM)
        mv_p = (mem_n @ w_xv[l]).reshape(b_p, n_mem, N_XHEADS, X_HEAD_DIM)
        s0 = jnp.zeros((b_p, N_RET_HEADS, RET_DK, RET_DV), jnp.float32)
        xp, k_new_p, v_new_p, s_p = decoder_layer(xp, pos_p, s0, None, None, mk_p, mv_p, *shared)
        k_past = cache_k[l][page_table].reshape(b_s, past_len, N_DIFF_HEADS, 2 * DIFF_D)
        v_past = cache_v[l][page_table].reshape(b_s, past_len, N_DIFF_HEADS, DIFF_DV)
        xs, k_new_s, v_new_s, s_s = decoder_layer(xs, pos_s, state_ret[l], k_past, v_past,
                                                  cache_mem_k[l], cache_mem_v[l], *shared)
        kp_l.append(k_new_p); vp_l.append(v_new_p); sp_l.append(s_p.astype(state_ret.dtype))
        mkp_l.append(mk_p); mvp_l.append(mv_p)
        ks_l.append(k_new_s); vs_l.append(v_new_s); ss_l.append(s_s.astype(state_ret.dtype))
    k_prompt = jnp.stack(kp_l)
    v_prompt = jnp.stack(vp_l)
    ret_prompt = jnp.stack(sp_l)
    mem_k_prompt = jnp.stack(mkp_l)
    mem_v_prompt = jnp.stack(mvp_l)
    k_sample = jnp.stack(ks_l)
    v_sample = jnp.stack(vs_l)
    ret_sample = jnp.stack(ss_l)
    return (xp, xs, k_prompt, v_prompt, ret_prompt, mem_k_prompt, mem_v_prompt, k_sample, v_sample, ret_sample)
```

```python
import numpy as np
import concourse.bass as bass
import concourse.mybir as mybir
from contextlib import ExitStack

F32 = mybir.dt.float32
BF16 = mybir.dt.bfloat16
I32 = mybir.dt.int32
ALU = mybir.AluOpType
AF = mybir.ActivationFunctionType
AX = mybir.AxisListType


class _TS:
    __slots__ = ("name", "writers", "readers", "sem", "semcnt", "dram")

    def __init__(self, name, dram=False):
        self.name = name
        self.writers = {}
        self.readers = {}
        self.sem = None
        self.semcnt = 0
        self.dram = dram


class Prog:
    def __init__(self, nc, es):
        self.nc = nc
        self.es = es
        self.ops = []
        self.ts = {}
        self.sems = {}
        self.cnt = {}
        self.waited = {e: {} for e in ("pe", "act", "dve", "pool", "sp")}
        for e in ("pe", "act", "dve", "pool"):
            self.sems["E" + e] = es.enter_context(nc.semaphore("sem_" + e))
            self.cnt["E" + e] = 0
        self.n_dma_sems = 0
        self.scope = es
        self.emitted = 0

    def sb(self, name, shape, dt):
        name = "s_" + name
        t = self.scope.enter_context(self.nc.sbuf_tensor(name, list(shape), dt))
        self.ts[name] = _TS(name)
        return t

    def ps(self, name, shape, dt=F32):
        name = "p_" + name
        t = self.scope.enter_context(self.nc.psum_tensor(name, list(shape), dt))
        self.ts[name] = _TS(name)
        return t

    def dram(self, name, shape, dt, kind="Internal"):
        name = "d_" + name
        t = self.nc.dram_tensor(name, list(shape), dt, kind=kind)
        self.ts[name] = _TS(name, dram=True)
        return t

    def _t(self, ap):
        n = ap.tensor.name
        if n not in self.ts:
            self.ts[n] = _TS(n, dram=True)
        return self.ts[n]

    def _tiles(self, aps):
        out = []
        for a in aps:
            if a is None or isinstance(a, (int, float)):
                continue
            t = self._t(a)
            if t not in out:
                out.append(t)
        return out

    def _deps(self, eng, reads, writes, tok):
        need = {}

        def add(tk):
            if tk is None:
                return
            k, v = tk
            if eng == "pe" and k == "Epe":
                return
            if need.get(k, 0) < v:
                need[k] = v

        for t in reads:
            for k, v in t.writers.items():
                add((k, v))
            if t.name.startswith("p_"):
                for k, v in t.readers.items():
                    if k != tok[0]:
                        add((k, v))
        for t in writes:
            if t.dram:
                continue
            for k, v in t.writers.items():
                add((k, v))
            for k, v in t.readers.items():
                add((k, v))
        waits = []
        w = self.waited[eng]
        for k, v in need.items():
            if w.get(k, 0) >= v:
                continue
            w[k] = v
            waits.append((k, v))
        for t in reads:
            if t in writes:
                continue
            if t.readers.get(tok[0], 0) < tok[1]:
                t.readers[tok[0]] = tok[1]
        for t in writes:
            if t.dram:
                if t.writers.get(tok[0], 0) < tok[1]:
                    t.writers[tok[0]] = tok[1]
            else:
                t.writers = {tok[0]: tok[1]}
                t.readers = {}
        return waits

    def op(self, eng, fn, reads, writes):
        reads = self._tiles(reads)
        writes = self._tiles(writes)
        k = "E" + eng
        self.cnt[k] += 1
        tok = (k, self.cnt[k])
        waits = self._deps(eng, reads, writes, tok)
        self.ops.append((eng, fn, waits, (k, 1)))

    def dma(self, q, out, in_, semap=None, fn=None, extra_reads=(), inc=16):
        if semap is None:
            semap = out if out.tensor.name in self.ts and self._is_chip(out) else in_
        st = self._t(semap)
        if st.sem is None:
            st.sem = "D" + st.name
            self.sems[st.sem] = self.es.enter_context(self.nc.semaphore("dsem_" + st.name))
            self.n_dma_sems += 1
        st.semcnt += inc
        tok = (st.sem, st.semcnt)
        reads = self._tiles([in_, *extra_reads])
        writes = self._tiles([out])
        waits = self._deps(q, reads, writes, tok)
        if fn is None:
            fn = lambda e, o=out, i=in_: e.dma_start(out=o, in_=i)
        self.ops.append((q, fn, waits, (st.sem, inc)))

    def _is_chip(self, ap):
        s = str(ap.space)
        return "DRAM" not in s.upper() and "HBM" not in s.upper()

    def barrier(self):
        allk = {k: v for k, v in self.cnt.items() if v > 0}
        for t in self.ts.values():
            if t.sem is not None and t.semcnt > 0:
                allk[t.sem] = t.semcnt
        for eng in ("pe", "act", "dve", "pool", "sp"):
            w = self.waited[eng]
            waits = []
            for k, v in allk.items():
                if k == "E" + eng:
                    continue
                if w.get(k, 0) >= v:
                    continue
                w[k] = v
                waits.append((k, v))
            self.ops.append((eng, None, waits, None))

    def final_wait(self, aps):
        reads = self._tiles(aps)
        waits = self._deps("sp", reads, [], ("none", 0))
        self.ops.append(("sp", None, waits, None))

    def emit(self):
        nc = self.nc
        ops = self.ops[self.emitted:]
        self.emitted = len(self.ops)
        sems = self.sems

        def run(engname, e):
            for (eng, fn, waits, inc) in ops:
                if eng != engname:
                    continue
                for (k, v) in waits:
                    e.wait_ge(sems[k], v)
                if fn is not None:
                    ins = fn(e)
                    if inc is not None:
                        ins.then_inc(sems[inc[0]], inc[1])

        with nc.Block() as block:
            @block.sync
            def _(e):
                run("sp", e)

            @block.scalar
            def _(e):
                run("act", e)

            @block.vector
            def _(e):
                run("dve", e)

            @block.gpsimd
            def _(e):
                run("pool", e)

            @block.tensor
            def _(e):
                run("pe", e)

    def mm(self, out, lhsT, rhs, start=True, stop=True, **kw):
        self.op("pe", lambda e: e.matmul(out, lhsT, rhs, start=start, stop=stop, **kw), [lhsT, rhs], [out])

    def tr(self, out, in_, ident):
        self.op("pe", lambda e: e.transpose(out, in_, ident), [in_, ident], [out])

    def act(self, out, in_, func, bias=0.0, scale=1.0, accum_out=None, eng="act"):
        rd = [in_, bias, scale]
        wr = [out, accum_out]
        kw = {}
        if accum_out is not None:
            kw["accum_out"] = accum_out
        self.op(eng, lambda e: e.activation(out, in_, func, bias=bias, scale=scale, **kw), rd, wr)

    def tt(self, eng, out, in0, in1, op):
        self.op(eng, lambda e: e.tensor_tensor(out, in0, in1, op), [in0, in1], [out])

    def ts_(self, eng, out, in0, s1, s2, op0, op1=None, accum_out=None):
        kw = {}
        if op1 is not None:
            kw["op1"] = op1
        if accum_out is not None:
            kw["accum_out"] = accum_out
        self.op(eng, lambda e: e.tensor_scalar(out, in0, s1, s2, op0, **kw), [in0, s1, s2], [out, accum_out])

    def stt(self, eng, out, in0, scalar, in1, op0, op1):
        self.op(eng, lambda e: e.scalar_tensor_tensor(out, in0, scalar, in1, op0, op1), [in0, scalar, in1], [out])

    def copy(self, eng, out, in_):
        if eng == "act":
            self.op(eng, lambda e: e.copy(out, in_), [in_], [out])
        else:
            self.op(eng, lambda e: e.tensor_copy(out, in_), [in_], [out])

    def memset(self, eng, out, val):
        self.op(eng, lambda e: e.memset(out, val), [], [out])

    def tss(self, eng, out, in_, scalar, op):
        self.op(eng, lambda e: e.tensor_single_scalar(out, in_, scalar, op), [in_, scalar], [out])

    def rsum(self, eng, out, in_):
        self.op(eng, lambda e: e.reduce_sum(out, in_, AX.X), [in_], [out])

    def recip(self, out, in_):
        self.op("dve", lambda e: e.reciprocal(out, in_), [in_], [out])

    def phase(self):
        return _Phase(self)


class _Phase:
    def __init__(self, P):
        self.P = P

    def __enter__(self):
        self.prev = self.P.scope
        self.st = ExitStack()
        self.st.__enter__()
        self.P.scope = self.st
        return self

    def __exit__(self, *a):
        stop = a[0] is not None and a[0].__name__ == "_Stop"
        if a[0] is None or stop:
            self.P.barrier()
            self.P.emit()
        self.P.scope = self.prev
        if stop:
            self.st.__exit__(None, None, None)
            return False
        return self.st.__exit__(*a)

from concourse.bass_utils import run_bass_kernel_spmd
import math

NCORE = 8
D = 2048
NTOK = 8448
NBLK = 66
TB = 1152
NEG = -100.0
LAM_INIT = 0.8 - 0.6 * math.exp(0.0)
EPS = 1e-6
DFF = 5632
NFF = 44


def _bucket_thresholds():
    n = np.arange(0, 700)
    nf = np.maximum(n, 1).astype(np.float32)
    large = 16 + (np.log(nf / np.float32(16)) / np.float32(math.log(128 / 16)) * np.float32(16)).astype(np.int32)
    large = np.minimum(large, 31)
    bk = np.where(n < 16, n, large)
    thr = [int(np.argmax(bk >= b)) for b in range(1, 32)]
    return thr


def _consts():
    c = {}
    c["ident"] = np.eye(128, dtype=np.float32)
    p = np.arange(128)
    c["maskp"] = (p[None, :] >= p[:, None]).astype(np.float32)
    c["masks"] = ((p[None, :] >= p[:, None]) & (p[None, :] // 8 == p[:, None] // 8)).astype(np.float32)
    dist = np.zeros((128, 1152), np.float32)
    x = np.arange(1024)
    dist[:, :1024] = x[None, :] - p[:, None] - 384
    same = (p[None, :] // 8 == p[:, None] // 8)
    dist[:, 1024:] = np.where(same, (p[None, :] % 8) - (p[:, None] % 8), -1000)
    c["dist"] = dist
    rm = np.zeros((128, 16), np.float32)
    rm[p, p // 8] = 1.0
    c["rowmask"] = rm
    dg = np.zeros((128, 128, 8), np.float32)
    dg[p, p, :] = 1.0
    c["dg"] = dg.reshape(128, 1024)
    e = np.zeros((128, 256), np.float32)
    e[:, 63] = 1.0
    e[:, 128 + 127] = 1.0
    c["esel"] = e
    return c


def _head_tables(h):
    inv_freq = (np.float32(10000.0) ** (-np.arange(0, 128, 2, dtype=np.float32) / np.float32(128))).astype(np.float32)
    log_g = np.log1p(-(2.0 ** (-5.0 - h)))
    p = np.arange(128)
    tabs = np.zeros((128, 33, 2, 2, 64), np.float32)
    for blk in range(33):
        if blk < 32:
            pos = (blk * 128 + p).astype(np.float32)
            li = p
        else:
            pos = (8192 + (p % 8)).astype(np.float32)
            li = p % 8
        ang = pos[:, None] * inv_freq[None, :]
        cs, sn = np.cos(ang).astype(np.float32), np.sin(ang).astype(np.float32)
        qd = np.exp(log_g * (li + 1.0))[:, None]
        kd = (np.exp(-log_g * (li + 1.0)) * 128.0 ** -0.5)[:, None]
        tabs[:, blk, 0, 0] = cs * qd
        tabs[:, blk, 0, 1] = sn * qd
        tabs[:, blk, 1, 0] = cs * kd
        tabs[:, blk, 1, 1] = sn * kd
    hc = np.zeros((128, 4), np.float32)
    hc[:, 0] = np.exp(log_g * 128.0)
    hc[:, 1] = np.exp(log_g * 8.0)
    return tabs.reshape(128, 33 * 256), hc


class _Stop(Exception):
    pass


class _SkipA(Exception):
    pass


class _BfView:
    def __init__(self, t):
        self.t = t
        self.v = t.ap().bitcast(BF16)

    def ap(self):
        return self.v

    def raw(self):
        return self.t.ap()

    def __getitem__(self, k):
        return self.v[k]


class _NullCtx:
    def __enter__(self):
        return self

    def __exit__(self, *a):
        return a[0] is not None and a[0].__name__ == "_SkipA"


def build_program(kstop=None, nphys=2560, part="AB"):
    holder = []
    try:
        return _build_inner(kstop, holder, nphys, part)
    except _Stop:
        return holder[0]


def _build_inner(kstop, holder, nphys=2560, part="AB"):
    nc = bass.Bass("TRN2", target_bir_lowering=False)
    holder.append(nc)
    small_a = kstop in ("A0", "A1", "A2")
    small_b = kstop in ("A0", "A1", "A2", "A3")
    es = ExitStack()
    with es:
        P = Prog(nc, es)
        C = _consts()
        thr = _bucket_thresholds()

        A_NAMES = ("x_all", "w_in", "tabs", "hc", "rb", "lamv", "gn", "state", "ck", "cv", "pt", "o_k", "o_v", "o_retp", "o_rets")
        B_NAMES = ("x_own", "gidx", "mem_own", "cmk", "cmv", "w_out", "w_xq", "w_xk", "w_xv", "w_xo", "w_gate", "w_up", "w_down",
                   "o_y", "o_mk", "o_mv", "o_xr")

        def _skip(name):
            return (part == "A" and name in B_NAMES) or (part == "B" and name in A_NAMES)

        def din(name, shape, dt=F32):
            if _skip(name):
                return None
            return nc.dram_tensor(name, list(shape), dt, kind="ExternalInput")

        def dout(name, shape, dt=F32):
            if _skip(name):
                return None
            return nc.dram_tensor(name, list(shape), dt, kind="ExternalOutput")

        x_all = din("x_all", [NTOK, D])
        w_in = din("w_in", [128, 16 * 896])
        grows = din("grows", [7, 128, D])
        tabs_d = din("tabs", [128, 33 * 256])
        hc_d = din("hc", [128, 4])
        rb_d = din("rb", [128, 32])
        lamv_d = din("lamv", [128, 256])
        gn_d = din("gn", [128, 256])
        state_d = din("state", [128, 32 * 128])
        ck_d = din("ck", [8 if small_a else (64 if kstop == "A3" else nphys), 16384])
        cv_d = din("cv", [8 if small_a else (64 if kstop == "A3" else nphys), 16384])
        pt_d = din("pt", [128, 16], I32)
        x_own = din("x_own", [TB, D])
        gidx_d = din("gidx", [128, 72], I32)
        mem_own = din("mem_own", [256, D])
        cmk_d = din("cmk", [4, 256, 512])
        cmv_d = din("cmv", [4, 256, 512])
        w_out = din("w_out", [128, 8 if small_b else 16 * 2048])
        w_xq = din("w_xq", [128, 8 if small_b else 16 * 512])
        w_xk = din("w_xk", [128, 8 if small_b else 16 * 512])
        w_xv = din("w_xv", [128, 8 if small_b else 16 * 512])
        w_xo = din("w_xo", [128, 8 if small_b else 4 * 2048])
        w_gate = din("w_gate", [1 if small_b else NFF, 128, 16 * 128])
        w_up = din("w_up", [1 if small_b else NFF, 128, 16 * 128])
        w_down = din("w_down", [1 if small_b else 16, 128, NFF * 128])

        o_k = dout("o_k", [NTOK, 128])
        o_v = dout("o_v", [NTOK, 128])
        o_retp = dout("o_retp", [2, 128, 128])
        o_rets = dout("o_rets", [32, 128, 128])
        o_y = dout("o_y", [TB, D])
        o_mk = dout("o_mk", [256, 512])
        o_mv = dout("o_mv", [256, 512])

        o_dbg = dout("o_dbg", [NTOK, 256], BF16) if (kstop is not None and part != "B") else None
        if part == "A":
            ag_in = dout("ag", [NTOK, 256], BF16)
        else:
            ag_in = _BfView(P.dram("ag_in", [NTOK, 128], F32))

        o_tw = dout("o_tw", [128, 1152]) if (kstop is not None and part != "B") else None
        dbgx = {}

        def dbgout(name, t, shape, dt):
            if kstop is None:
                return
            o = dout("o_x_" + name, [shape[0], int(np.prod(shape[1:]))], dt)
            nd = len(shape)
            letters = "abcdefg"[:nd - 1]
            src = t[:] if nd == 2 else t[:].rearrange("p %s -> p (%s)" % (" ".join(letters), " ".join(letters)))
            P.dma("sp", o.ap(), src)

        o_xr = dout("o_xr", [TB, D]) if (kstop is not None and part != "A") else None

        def dbg_xr():
            if o_xr is None:
                return
            P.dma("sp", o_xr.ap(), xr_d.ap(), semap=idf[:])

        def dbg_dump():
            if o_dbg is not None and part != "B":
                P.dma("sp", o_dbg.ap(), ag_in.ap(), semap=idf[:])
                P.dma("sp", o_tw.ap(), dbgx["Tw"][:])

        if part == "B":
            ag_out = din("ag_all", [NCORE * NTOK, 256], BF16)
        else:
            ag_out = _BfView(P.dram("ag_out", [NCORE * NTOK, 128], F32))
        xr_d = P.dram("xr_d", [TB, D], F32)

        cd = {k: nc.inline_tensor(v, "c_" + k) for k, v in C.items()}

        idf = P.sb("idf", [128, 128], F32)
        idb = P.sb("idb", [128, 128], BF16)
        epsT = P.sb("epsT", [128, 1], F32)
        P.dma("sp", idf[:], cd["ident"].ap())
        P.copy("dve", idb[:], idf[:])
        P.memset("dve", epsT[:], EPS)
        c8 = P.sb("c8", [128, 1], F32)
        P.memset("dve", c8[:], 0.125)

        def rstd_of(out, ss, n):
            P.act(out, ss, AF.Sqrt, bias=epsT[:, 0:1], scale=1.0 / n)
            P.recip(out, out)

        if part == "B":
            _phaseA = _NullCtx()
        else:
            _phaseA = P.phase()
        with _phaseA:
            if part == "B":
                raise _SkipA()
            hc = P.sb("hc", [128, 4], F32)
            P.dma("sp", hc[:], hc_d.ap())
            rb = P.sb("rb", [128, 32], F32)
            P.dma("sp", rb[:], rb_d.ap())
            lamv = P.sb("lamv", [128, 256], F32)
            P.dma("sp", lamv[:], lamv_d.ap())
            gn = P.sb("gn", [128, 256], F32)
            P.dma("sp", gn[:], gn_d.ap())
            dist = P.sb("dist", [128, 1152], F32)
            P.dma("sp", dist[:], cd["dist"].ap())

            lt = P.sb("lt", [128, 128], F32)
            ls = P.sb("ls", [128, 4], F32)
            P.tt("dve", lt[:, 0:64], lamv[:, 0:64], lamv[:, 64:128], ALU.mult)
            P.tt("dve", lt[:, 64:128], lamv[:, 128:192], lamv[:, 192:256], ALU.mult)
            P.rsum("dve", ls[:, 0:1], lt[:, 0:64])
            P.rsum("dve", ls[:, 1:2], lt[:, 64:128])
            P.act(ls[:, 0:2], ls[:, 0:2], AF.Exp)
            neglam = P.sb("neglam", [128, 1], F32)
            P.tt("dve", ls[:, 2:3], ls[:, 1:2], ls[:, 0:1], ALU.subtract)
            P.tss("dve", neglam[:], ls[:, 2:3], -LAM_INIT, ALU.add)
            gnd = P.sb("gnd", [128, 128], F32)
            P.tss("dve", gnd[:], gn[:, 128:256], 1.0 - LAM_INIT, ALU.mult)

            Tw = P.sb("Tw", [128, 1152], F32)
            twt = P.sb("twt", [128, 1152], F32)
            dl = P.sb("dl", [128, 31], F32)
            P.tt("pool", dl[:], rb[:, 1:32], rb[:, 0:31], ALU.subtract)
            P.ts_("pool", Tw[:], dist[:], 0.0, rb[:, 0:1], ALU.mult, ALU.add)
            for b in range(1, 32):
                P.ts_("pool", twt[:], dist[:], float(thr[b - 1]), dl[:, b - 1:b], ALU.is_ge, ALU.mult)
                P.tt("pool", Tw[:], Tw[:], twt[:], ALU.add)
            P.ts_("pool", twt[:], dist[:], 0.0, NEG, ALU.is_lt, ALU.mult)
            P.tt("pool", Tw[:], Tw[:], twt[:], ALU.add)
            c31 = rb[:, 31:32]
            dbgx["Tw"] = Tw

            dqT = P.sb("dqT", [128, NTOK], BF16)
            dkT = P.sb("dkT", [128, NTOK], BF16)
            Vaug = P.sb("Vaug", [128, NBLK, 129], BF16)
            P.memset("pool", Vaug[:, :, 128:129], 1.0)

            with P.phase():
                wi = P.sb("wi", [128, 16, 896], BF16)
                P.dma("pool", wi[:].rearrange("p a b -> p (a b)"), w_in.ap())
                tabl = [P.sb("tabl%d" % i, [128, 2, 2, 64], F32) for i in range(2)]
                gpre = P.sb("gpre", [128, D], F32)
                P.dma("sp", gpre[:], grows[0])
                maskp = P.sb("maskp", [128, 128], F32)
                masks = P.sb("masks", [128, 128], F32)
                P.dma("sp", maskp[:], cd["maskp"].ap())
                P.dma("sp", masks[:], cd["masks"].ap())
                rowmask = P.sb("rowmask", [128, 16], F32)
                P.dma("sp", rowmask[:], cd["rowmask"].ap())
                S0b = P.sb("S0b", [128, 32, 128], BF16)
                P.dma("pool", S0b[:].rearrange("p a b -> p (a b)"), state_d.ap())
                s0f = [P.sb("s0f%d" % i, [128, 128], F32) for i in range(2)]
                xin = [P.sb("xin%d" % i, [128, D], F32) for i in range(2)]
                xsb = [P.sb("xsb%d" % i, [128, D], BF16) for i in range(2)]
                hT = [P.sb("hT%d" % i, [128, 16, 128], BF16) for i in range(2)]
                junk = P.sb("junk", [128, D], BF16)
                ss = P.sb("ss", [128, 8], F32)
                pr = [P.sb("pr%d" % i, [128, 896], F32) for i in range(2)]
                cb = [P.sb("cb%d" % i, [128, 128], BF16) for i in range(2)]
                dqk = [P.sb("dqk%d" % i, [128, 256], BF16) for i in range(2)]
                qk = [P.sb("qk%d" % i, [128, 2, 128], BF16) for i in range(2)]
                qkT = [P.sb("qkT%d" % i, [128, 2, 128], BF16) for i in range(2)]
                rt = [P.sb("rt%d" % i, [128, 2, 64], F32) for i in range(4)]
                gs = [P.sb("gs%d" % i, [128, 128], F32) for i in range(2)]
                scT = [P.sb("scT%d" % i, [128, 128], BF16) for i in range(2)]
                ro = [P.sb("ro%d" % i, [128, 128], BF16) for i in range(2)]
                ug = P.sb("ug", [128, 128], F32)
                S = P.sb("S", [128, 128], F32)
                Sb = P.sb("Sb", [128, 128], BF16)
                crT = P.sb("crT", [128, 128], BF16)
                km = [P.sb("km%d" % i, [128, 128], BF16) for i in range(2)]
                sn = [P.sb("sn%d" % i, [128, 128], F32) for i in range(2)]
                ptr = [P.ps("ptr%d" % i, [128, 1024], BF16) for i in range(2)]
                pp = [P.ps("pp%d" % i, [128, 512], F32) for i in range(2)]
                ptq = P.ps("ptq", [128, 1024], BF16)
                psc = P.ps("psc", [128, 512], F32)
                po = P.ps("po", [128, 512], F32)
                pu = P.ps("pu", [128, 512], F32)
                P.memset("dve", S[:], 0.0)
                P.memset("dve", Sb[:], 0.0)

                P.barrier()
                if kstop == "A0":
                    raise _Stop()
                import os

                def ckp(n):
                    if kstop == "A1" and int(os.environ.get("STEP", "99")) == n:
                        raise _Stop()
                for g in range(NBLK):
                    if kstop == "A1" and g >= int(os.environ.get("NBD", "66")):
                        break
                    i2 = g % 2
                    samp = g >= 64
                    tb = 32 if samp else g % 32
                    r0 = g * 128
                    x_t, xs_t, hT_t, pr_t = xin[i2], xsb[i2], hT[i2], pr[i2]
                    P.dma("sp", x_t[:], x_all[r0:r0 + 128, :])
                    tabs_t = tabl[i2]
                    P.dma("sp", tabs_t[:].rearrange("p a b c -> p (a b c)"), tabs_d[:, tb * 256:(tb + 1) * 256])
                    ckp(0)
                    P.memset("dve", ss[:, 0:2], 0.0)
                    P.act(junk[:], x_t[:], AF.Square, accum_out=ss[:, 0:1])
                    ckp(11)
                    rstd_of(ss[:, 2:3], ss[:, 0:1], D)
                    ckp(12)
                    P.stt("dve", xs_t[:], x_t[:], ss[:, 2:3], gpre[:], ALU.mult, ALU.mult)
                    ckp(1)
                    for kc in range(16):
                        P.tr(ptr[kc // 8][:, (kc % 8) * 128:(kc % 8 + 1) * 128], xs_t[:, kc * 128:(kc + 1) * 128], idb[:])
                    P.copy("dve", hT_t[:, 0:8, :].rearrange("p a b -> p (a b)"), ptr[0][:])
                    P.copy("act", hT_t[:, 8:16, :].rearrange("p a b -> p (a b)"), ptr[1][:])
                    ckp(2)
                    for ng, (c0, c1) in enumerate([(0, 512), (512, 896)]):
                        for kc in range(16):
                            P.mm(pp[ng][:, 0:c1 - c0], hT_t[:, kc, :], wi[:, kc, c0:c1], start=(kc == 0), stop=(kc == 15))
                    P.copy("act", pr_t[:, 0:512], pp[0][:])
                    P.copy("dve", pr_t[:, 512:896], pp[1][:, 0:384])
                    ckp(3)
                    P.dma("sp", o_k[r0:r0 + 128, :], pr_t[:, 640:768])
                    P.dma("sp", o_v[r0:r0 + 128, :], pr_t[:, 768:896])
                    cb_t, dqk_t, qk_t, qkT_t, gs_t = cb[i2], dqk[i2], qk[i2], qkT[i2], gs[i2]
                    P.copy("pool", cb_t[:], pr_t[:, 256:384])
                    P.copy("pool", dqk_t[:], pr_t[:, 512:768])
                    P.copy("pool", Vaug[:, g, 0:128], pr_t[:, 768:896])
                    ckp(4)
                    r4 = pr_t[:, 0:256].rearrange("p (a d two) -> p a d two", a=2, two=2)
                    o4 = qk_t[:].rearrange("p a (d two) -> p a d two", two=2)
                    x1, x2 = r4[:, :, :, 0], r4[:, :, :, 1]
                    cT, sT = tabs_t[:, :, 0, :], tabs_t[:, :, 1, :]
                    P.tt("dve", rt[0][:], x1, cT, ALU.mult)
                    P.tt("pool", rt[1][:], x2, sT, ALU.mult)
                    P.tt("dve", o4[:, :, :, 0], rt[0][:], rt[1][:], ALU.subtract)
                    P.tt("pool", rt[2][:], x1, sT, ALU.mult)
                    P.tt("dve", rt[3][:], x2, cT, ALU.mult)
                    P.tt("pool", o4[:, :, :, 1], rt[2][:], rt[3][:], ALU.add)
                    ckp(5)
                    P.tr(ptq[:, 0:128], qk_t[:, 0, :], idb[:])
                    P.tr(ptq[:, 128:256], qk_t[:, 1, :], idb[:])
                    P.tr(ptq[:, 256:384], dqk_t[:, 0:128], idb[:])
                    P.tr(ptq[:, 384:512], dqk_t[:, 128:256], idb[:])
                    P.copy("act", qkT_t[:].rearrange("p a b -> p (a b)"), ptq[:, 0:256])
                    P.copy("dve", dqT[:, r0:r0 + 128], ptq[:, 256:384])
                    P.copy("act", dkT[:, r0:r0 + 128], ptq[:, 384:512])
                    ckp(6)
                    P.act(gs_t[:], pr_t[:, 384:512], AF.Silu)
                    P.tt("pool", gs_t[:], gs_t[:], gn[:, 0:128], ALU.mult)
                    ckp(7)
                    scT_t, ro_t = scT[i2], ro[i2]
                    P.mm(psc[:, 0:128], qkT_t[:, 1, :], qkT_t[:, 0, :])
                    P.tt("dve", scT_t[:], psc[:, 0:128], (masks if samp else maskp)[:], ALU.mult)
                    ckp(8)
                    if not samp:
                        P.mm(po[:, 0:128], scT_t[:], cb_t[:], start=True, stop=False)
                        P.mm(po[:, 0:128], qkT_t[:, 0, :], Sb[:], start=False, stop=True)
                    else:
                        sbi = g - 64
                        for bb in range(16):
                            P.mm(psc[:, 128 + bb * 8:128 + bb * 8 + 8], S0b[:, sbi * 16 + bb, :], qkT_t[:, 0, bb * 8:bb * 8 + 8])
                        P.copy("act", crT[:], psc[:, 128:256])
                        P.mm(po[:, 0:128], scT_t[:], cb_t[:], start=True, stop=False)
                        P.mm(po[:, 0:128], crT[:], idb[:], start=False, stop=True)
                    P.act(junk[:, 0:128], po[:, 0:128], AF.Square, accum_out=ss[:, 1:2])
                    rstd_of(ss[:, 3:4], ss[:, 1:2], 128)
                    P.stt("dve", ro_t[:], po[:, 0:128], ss[:, 3:4], gs_t[:], ALU.mult, ALU.mult)
                    ckp(9)
                    P.dma("sp", ag_in[r0:r0 + 128, 0:128], ro_t[:])
                    ckp(10)
                    if not samp:
                        P.mm(pu[:, 0:128], qk_t[:, 1, :], cb_t[:])
                        P.tss("dve", ug[:], pu[:, 0:128], hc[:, 0:1], ALU.mult)
                        P.stt("dve", S[:], S[:], hc[:, 0:1], ug[:], ALU.mult, ALU.add)
                        P.copy("act", Sb[:], S[:])
                        if tb == 31:
                            P.dma("sp", o_retp[g // 32], S[:])
                            P.memset("dve", S[:], 0.0)
                            P.memset("dve", Sb[:], 0.0)
                    else:
                        sbi = g - 64
                        for bb in range(16):
                            b = sbi * 16 + bb
                            km_t, sn_t = km[bb % 2], sn[bb % 2]
                            P.tss("pool", km_t[:], qk_t[:, 1, :], rowmask[:, bb:bb + 1], ALU.mult)
                            P.mm(pu[:, (bb % 2) * 128:(bb % 2) * 128 + 128], km_t[:], cb_t[:])
                            P.tss("dve", ug[:], pu[:, (bb % 2) * 128:(bb % 2) * 128 + 128], hc[:, 1:2], ALU.mult)
                            P.dma("sp", s0f[bb % 2][:], state_d[:, b * 128:(b + 1) * 128])
                            P.stt("dve", sn_t[:], s0f[bb % 2][:], hc[:, 1:2], ug[:], ALU.mult, ALU.add)
                            P.dma("sp", o_rets[b], sn_t[:])

            if kstop == "A1":
                dbg_dump()
                raise _Stop()
            with P.phase():
                pss = [[P.ps("pss%d%d" % (m, i), [128, 512], F32) for i in range(2)] for m in range(2)]
                pob_ = [P.ps("pob%d" % i, [128, 512], F32) for i in range(4)]
                pob = [t[:, 0:258].rearrange("p (m d) -> p m d", m=2) for t in pob_]
                pT = [[P.sb("pT%d%d" % (m, i), [128, 512], BF16) for i in range(2)] for m in range(2)]
                ntmp = [P.sb("ntmp%d" % i, [128, 512], F32) for i in range(2)]
                fz = P.sb("fz", [128, 8], F32)
                a2 = [P.sb("a2_%d" % i, [128, 128], F32) for i in range(2)]
                dif = [P.sb("dif%d" % i, [128, 128], F32) for i in range(2)]
                dob = [P.sb("dob%d" % i, [128, 128], BF16) for i in range(2)]
                jk2 = P.sb("jk2", [128, 128], BF16)
                cnt = 0
                for b in range(2):
                    for Q in range(8):
                        q0 = b * 4096 + Q * 512
                        jmax = 4 * Q + 3
                        for j in range(jmax + 1):
                            k0 = b * 4096 + j * 128
                            near = j >= 4 * Q - 1
                            bi = cnt % 2
                            cnt += 1
                            for m in range(2):
                                P.mm(pss[m][bi][:], dkT[64 * m:64 * m + 64, k0:k0 + 128], dqT[64 * m:64 * m + 64, q0:q0 + 512])
                                if not near:
                                    P.act(pT[m][bi][:], pss[m][bi][:], AF.Exp, bias=c31, scale=0.125)
                                else:
                                    x0 = 384 + 512 * Q - 128 * j
                                    P.stt("dve", ntmp[m][:], pss[m][bi][:], c8[:, 0:1], Tw[:, x0:x0 + 512], ALU.mult, ALU.add)
                                    P.act(pT[m][bi][:], ntmp[m][:], AF.Exp)
                            for qb in range(4):
                                if j > 4 * Q + qb:
                                    continue
                                for m in range(2):
                                    P.mm(pob[qb][:, m, :], pT[m][bi][:, qb * 128:(qb + 1) * 128], Vaug[:, b * 32 + j, :],
                                         start=(j == 0 and m == 0), stop=(j == 4 * Q + qb and m == 1), skip_group_check=True)
                        for qb in range(4):
                            i2 = qb % 2
                            P.recip(fz[:, 0:1], pob[qb][:, 0, 128:129])
                            P.recip(fz[:, 1:2], pob[qb][:, 1, 128:129])
                            P.tt("dve", fz[:, 2:3], fz[:, 1:2], neglam[:], ALU.mult)
                            P.tss("dve", a2[i2][:], pob[qb][:, 1, 0:128], fz[:, 2:3], ALU.mult)
                            P.stt("dve", dif[i2][:], pob[qb][:, 0, 0:128], fz[:, 0:1], a2[i2][:], ALU.mult, ALU.add)
                            P.memset("dve", fz[:, 3:4], 0.0)
                            P.act(jk2[:], dif[i2][:], AF.Square, accum_out=fz[:, 3:4])
                            rstd_of(fz[:, 4:5], fz[:, 3:4], 128)
                            P.stt("dve", dob[i2][:], dif[i2][:], fz[:, 4:5], gnd[:], ALU.mult, ALU.mult)
                            r0 = q0 + qb * 128
                            P.dma("sp", ag_in[r0:r0 + 128, 128:256], dob[i2][:])

            if kstop == "A2":
                dbg_dump()
                raise _Stop()
            with P.phase():
                pts = P.sb("pts", [128, 16], I32)
                P.dma("sp", pts[:], pt_d.ap())
                dg = P.sb("dg", [128, 1024], F32)
                P.dma("sp", dg[:], cd["dg"].ap())
                esel = P.sb("esel", [128, 256], F32)
                P.dma("sp", esel[:], cd["esel"].ap())
                pkt = [P.ps("pkt%d" % i, [128, 512], F32) for i in range(2)]
                psq = [P.ps("psq%d" % i, [128, 512], F32) for i in range(2)]
                pom = [P.ps("pom%d" % i, [128, 512], F32) for i in range(2)]
                psn = [P.ps("psn%d" % i, [128, 512], F32) for i in range(2)]
                G = P.sb("G", [128, 8], F32)
                P.tss("dve", G[:], Tw[:, 512:520], c31, ALU.subtract)
                P.act(G[:], G[:], AF.Exp)
                P.tss("dve", G[:], G[:], -1.0, ALU.add)
                dgg = P.sb("dgg", [128, 128, 8], F32)
                P.tt("dve", dgg[:], dg[:].rearrange("p (s q) -> p s q", q=8), G[:].unsqueeze(1).to_broadcast([128, 128, 8]), ALU.mult)
                corr = P.sb("corr", [128, 128, 2, 2, 8], BF16)
                P.memset("pool", corr[:].rearrange("p s m a q -> p (s m a q)"), 1.0)
                for ab in range(2):
                    for hh in range(2):
                        P.mm(psn[hh][:], esel[:, ab * 128:(ab + 1) * 128], dgg[:, hh * 64:(hh + 1) * 64, :].rearrange("p s q -> p (s q)"))
                        for m in range(2):
                            P.tss("dve", corr[:, hh * 64:(hh + 1) * 64, m, ab, :], psn[hh][:].rearrange("p (s q) -> p s q", q=8), 1.0, ALU.add)
                Qbd = P.sb("Qbd", [128, 16, 2, 16], BF16)
                P.memset("pool", Qbd[:].rearrange("p a m q -> p (a m q)"), 0.0)
                P.copy("dve", Qbd[0:64, :, 0, :], dqT[0:64, 8192:8448].rearrange("p (a q) -> p a q", q=16))
                P.copy("dve", Qbd[64:128, :, 1, :], dqT[64:128, 8192:8448].rearrange("p (a q) -> p a q", q=16))
                pnew = [P.sb("pnew%d" % i, [128, 2, 128], BF16) for i in range(2)]
                ntm = P.sb("ntm", [128, 128], F32)
                for sblk in range(2):
                    t0 = 8192 + sblk * 128
                    for m in range(2):
                        P.mm(psn[m][:, 0:128], dkT[64 * m:64 * m + 64, t0:t0 + 128], dqT[64 * m:64 * m + 64, t0:t0 + 128])
                        P.stt("dve", ntm[:], psn[m][:, 0:128], c8[:, 0:1], Tw[:, 1024:1152], ALU.mult, ALU.add)
                        P.act(pnew[sblk][:, m, :], ntm[:], AF.Exp)
                kq = [P.sb("kq%d" % i, [128, 4096], F32) for i in range(2)]
                vq = [P.sb("vq%d" % i, [128, 32, 129], BF16) for i in range(2)]
                vqf = P.sb("vqf", [128, 4096], F32)
                kT = [P.sb("kT%d" % i, [128, 32, 128], BF16) for i in range(2)]
                pq = [P.sb("pq%d" % i, [128, 32, 2, 2, 8], BF16) for i in range(2)]
                ost = P.sb("ost", [16, 16, 2, 129], F32)
                for i in range(2):
                    P.memset("pool", vq[i][:, :, 128:129], 1.0)
                    P.memset("pool", pq[i][:].rearrange("p s m a q -> p (s m a q)"), 0.0)
                u = 0
                for p in range(16):
                    for qt in range(4):
                        i2 = u % 2
                        u += 1
                        kq_t, vq_t, kT_t, pq_t = kq[i2], vq[i2], kT[i2], pq[i2]
                        P.dma("pool", kq_t[:], ck_d.ap(), extra_reads=[pts[:]],
                              fn=lambda e, o=kq_t[:], i=ck_d.ap(), ix=pts[:, p:p + 1], eo=qt * 4096:
                              e.indirect_dma_start(out=o, out_offset=None, in_=i, in_offset=bass.IndirectOffsetOnAxis(ap=ix, axis=0), element_offset=eo))
                        P.dma("pool", vqf[:], cv_d.ap(), extra_reads=[pts[:]],
                              fn=lambda e, o=vqf[:], i=cv_d.ap(), ix=pts[:, p:p + 1], eo=qt * 4096:
                              e.indirect_dma_start(out=o, out_offset=None, in_=i, in_offset=bass.IndirectOffsetOnAxis(ap=ix, axis=0), element_offset=eo))
                        P.copy("pool", vq_t[:, :, 0:128], vqf[:].rearrange("p (s d) -> p s d", d=128))
                        for s4 in range(8):
                            bk = pkt[s4 % 2]
                            for si in range(4):
                                s = s4 * 4 + si
                                P.tr(bk[:, si * 128:(si + 1) * 128], kq_t[:, s * 128:(s + 1) * 128], idf[:])
                            P.copy("act" if s4 % 2 else "dve", kT_t[:, s4 * 4:s4 * 4 + 4, :].rearrange("p a b -> p (a b)"), bk[:])
                        for s in range(32):
                            P.mm(psq[s // 16][:, (s % 16) * 32:(s % 16) * 32 + 32], kT_t[:, s, :],
                                 Qbd[:, p, :, :].rearrange("p m q -> p (m q)"))
                        for hh in range(2):
                            src = psq[hh][:].rearrange("p (s m a q) -> p s m a q", m=2, a=2, q=8)
                            P.act(pq_t[0:64, hh * 16:(hh + 1) * 16, :, 0, :], src[0:64, :, :, 0, :], AF.Exp, bias=rb[0:64, 31:32], scale=0.125)
                            P.act(pq_t[64:128, hh * 16:(hh + 1) * 16, :, 1, :], src[64:128, :, :, 1, :], AF.Exp, bias=rb[64:128, 31:32], scale=0.125)
                        P.tt("dve", pq_t[:].rearrange("p s m a q -> p (s m a q)"), pq_t[:].rearrange("p s m a q -> p (s m a q)"),
                             corr[:, qt * 32:(qt + 1) * 32, :, :, :].rearrange("p s m a q -> p (s m a q)"), ALU.mult)
                        for s in range(32):
                            for m in range(2):
                                P.mm(pom[m][0:16, 0:129], pq_t[:, s, m, :, :].rearrange("p a q -> p (a q)"), vq_t[:, s, :],
                                     start=(qt == 0 and s == 0), stop=False)
                    sblk, c0 = p // 8, (p % 8) * 16
                    for m in range(2):
                        P.mm(pom[m][0:16, 0:129], pnew[sblk][:, m, c0:c0 + 16], Vaug[:, 64 + sblk, :], start=False, stop=True)
                        P.copy("dve", ost[:, p, m, :], pom[m][0:16, 0:129])
                dbgout("ost", ost, [16, 16, 2, 129], F32)
                dbgout("corr", corr, [128, 128, 2, 2, 8], BF16)
                dbgout("pq", pq[1], [128, 32, 2, 2, 8], BF16)
                dbgout("vq", vq[1], [128, 32, 129], BF16)
                dbgout("kT", kT[1], [128, 32, 128], BF16)
                dbgout("Qbd", Qbd, [128, 16, 2, 16], BF16)
                dbgout("pnew", pnew[1], [128, 2, 128], BF16)
                rr = P.sb("rr", [16, 16, 2], F32)
                P.recip(rr[:], ost[:, :, :, 128])
                P.tss("dve", rr[:, :, 1], rr[:, :, 1], neglam[0:16, 0:1], ALU.mult)
                sa = P.sb("sa", [16, 128], F32)
                sd = P.sb("sd", [16, 16, 128], F32)
                ssd = P.sb("ssd", [16, 32], F32)
                sjk = P.sb("sjk", [16, 128], BF16)
                sob = P.sb("sob", [16, 16, 128], BF16)
                P.memset("dve", ssd[:], 0.0)
                for p in range(16):
                    P.tss("dve", sa[:], ost[:, p, 1, 0:128], rr[:, p, 1:2], ALU.mult)
                    P.stt("dve", sd[:, p, :], ost[:, p, 0, 0:128], rr[:, p, 0:1], sa[:], ALU.mult, ALU.add)
                    P.act(sjk[:], sd[:, p, :], AF.Square, accum_out=ssd[:, p:p + 1])
                P.act(ssd[:, 16:32], ssd[:, 0:16], AF.Sqrt, bias=epsT[0:16, 0:1], scale=1.0 / 128)
                P.recip(ssd[:, 16:32], ssd[:, 16:32])
                for p in range(16):
                    P.stt("dve", sob[:, p, :], sd[:, p, :], ssd[:, 16 + p:17 + p], gnd[0:16, :], ALU.mult, ALU.mult)
                    P.dma("sp", ag_in[8192 + 16 * p:8192 + 16 * p + 16, 128:256], sob[:, p, :])

        if kstop == "A3":
            dbg_dump()
            P.barrier()
            P.emit()
            raise _Stop()
        if part == "A":
            P.final_wait([o_k.ap(), o_v.ap(), o_retp.ap(), o_rets.ap(), ag_in.ap()])
            P.emit()
            raise _Stop()
        if part == "AB":
            P.dma("pool", ag_out.ap(), ag_in.ap(), semap=ag_out.ap(), inc=1,
                  fn=lambda e: e.collective_compute("AllGather", ALU.bypass, replica_groups=[list(range(NCORE))],
                                                    ins=[ag_in.raw().opt()], outs=[ag_out.raw().opt()]))

        def load_grow(t, k):
            P.dma("sp", t[:], grows[k])

        with P.phase():
            if True:
                h3T = P.sb("h3T", [128, 16, TB], BF16)
                with P.phase():
                    wo = P.sb("wo", [128, 16, 2048], BF16)
                    P.dma("pool", wo[:].rearrange("p a b -> p (a b)"), w_out.ap())
                    gix = P.sb("gix", [128, 72], I32)
                    P.dma("sp", gix[:], gidx_d.ap())
                    gpo = P.sb("gpo", [128, D], F32)
                    load_grow(gpo, 1)
                    mix = [P.sb("mix%d" % i, [128, D], BF16) for i in range(1)] * 2
                    mixT = [P.sb("mixT%d" % i, [128, 16, 128], BF16) for i in range(1)] * 2
                    xo = [P.sb("xo%d" % i, [128, D], F32) for i in range(1)] * 2
                    x1 = [P.sb("x1_%d" % i, [128, D], F32) for i in range(1)] * 2
                    jb = P.sb("jb", [128, 512], BF16)
                    sb1 = P.sb("sb1", [128, 8], F32)
                    ptr = [P.ps("b_ptr%d" % i, [128, 1024], BF16) for i in range(2)]
                    pb = [P.ps("b_pb%d" % i, [128, 512], F32) for i in range(4)]
                    for blk in range(9):
                        i2 = blk % 2
                        for h in range(8):
                            P.dma("pool", mix[i2][:], ag_out.ap(), extra_reads=[gix[:]],
                                  fn=lambda e, o=mix[i2][:, h * 256:(h + 1) * 256], ix=gix[:, blk * 8 + h:blk * 8 + h + 1]:
                                  e.indirect_dma_start(out=o, out_offset=None, in_=ag_out.ap(), in_offset=bass.IndirectOffsetOnAxis(ap=ix, axis=0)))
                        P.dma("sp", xo[i2][:], x_own[blk * 128:(blk + 1) * 128, :])
                        for kc in range(16):
                            P.tr(ptr[kc // 8][:, (kc % 8) * 128:(kc % 8 + 1) * 128], mix[i2][:, kc * 128:(kc + 1) * 128], idb[:])
                        P.copy("dve", mixT[i2][:, 0:8, :].rearrange("p a b -> p (a b)"), ptr[0][:])
                        P.copy("act", mixT[i2][:, 8:16, :].rearrange("p a b -> p (a b)"), ptr[1][:])
                        P.memset("dve", sb1[:, 0:4], 0.0)
                        for ng in range(4):
                            for kc in range(16):
                                P.mm(pb[ng][:], mixT[i2][:, kc, :], wo[:, kc, ng * 512:(ng + 1) * 512], start=(kc == 0), stop=(kc == 15))
                            P.act(jb[:], pb[ng][:], AF.Square, accum_out=sb1[:, ng:ng + 1])
                        P.rsum("dve", sb1[:, 4:5], sb1[:, 0:4])
                        rstd_of(sb1[:, 5:6], sb1[:, 4:5], D)
                        for ng in range(4):
                            sl = slice(ng * 512, (ng + 1) * 512)
                            P.stt("dve", x1[i2][:, sl], pb[ng][:], sb1[:, 5:6], gpo[:, sl], ALU.mult, ALU.mult)
                        P.tt("pool", x1[i2][:], x1[i2][:], xo[i2][:], ALU.add)
                        P.dma("sp", xr_d[blk * 128:(blk + 1) * 128, :], x1[i2][:])

                if kstop == "B1a":
                    dbg_dump()
                    dbg_xr()
                    raise _Stop()
                with P.phase():
                    wq = P.sb("wq", [128, 16, 512], BF16)
                    P.dma("pool", wq[:].rearrange("p a b -> p (a b)"), w_xq.ap())
                    wxo = P.sb("wxo", [128, 4, 2048], BF16)
                    P.dma("pool", wxo[:].rearrange("p a b -> p (a b)"), w_xo.ap())
                    MKT = P.sb("MKT", [128, 5, 4, 256], BF16)
                    MVa = P.sb("MVa", [128, 5, 2, 4, 129], BF16)
                    P.memset("pool", MVa[:, :, :, :, 128:129].rearrange("p a b c d -> p (a b c d)"), 1.0)
                    ptr = [P.ps("c_ptr%d" % i, [128, 1024], BF16) for i in range(2)]
                    pb = [P.ps("c_pb%d" % i, [128, 512], F32) for i in range(4)]
                    pqx = P.ps("c_pq", [128, 512], F32)
                    ptm = P.ps("c_ptm", [128, 1024], BF16)
                    gA = P.sb("gA", [128, D], F32)
                    gB = P.sb("gB", [128, D], F32)
                    sb2 = P.sb("sb2", [128, 12], F32)
                    jb = P.sb("jb2", [128, D], BF16)
                    xa = [P.sb("xa%d" % i, [128, D], F32) for i in range(2)]
                    xsb = [P.sb("xsb2_%d" % i, [128, D], BF16) for i in range(2)]
                    hT = [P.sb("hT2_%d" % i, [128, 16, 128], BF16) for i in range(2)]
                    mkf = P.sb("mkf", [128, 512], F32)
                    mkb = P.sb("mkb", [128, 512], BF16)

                    with P.phase():
                        wk = P.sb("wk", [128, 16, 512], BF16)
                        wv = P.sb("wv", [128, 16, 512], BF16)
                        P.dma("pool", wk[:].rearrange("p a b -> p (a b)"), w_xk.ap())
                        P.dma("pool", wv[:].rearrange("p a b -> p (a b)"), w_xv.ap())
                        load_grow(gA, 6)
                        for mb in range(2):
                            P.dma("sp", xa[mb][:], mem_own[mb * 128:(mb + 1) * 128, :])
                            P.memset("dve", sb2[:, 0:1], 0.0)
                            P.act(jb[:], xa[mb][:], AF.Square, accum_out=sb2[:, 0:1])
                            rstd_of(sb2[:, 1:2], sb2[:, 0:1], D)
                            P.stt("dve", xsb[mb][:], xa[mb][:], sb2[:, 1:2], gA[:], ALU.mult, ALU.mult)
                            for kc in range(16):
                                P.tr(ptr[kc // 8][:, (kc % 8) * 128:(kc % 8 + 1) * 128], xsb[mb][:, kc * 128:(kc + 1) * 128], idb[:])
                            P.copy("dve", hT[mb][:, 0:8, :].rearrange("p a b -> p (a b)"), ptr[0][:])
                            P.copy("act", hT[mb][:, 8:16, :].rearrange("p a b -> p (a b)"), ptr[1][:])
                            for kc in range(16):
                                P.mm(pb[0][:], hT[mb][:, kc, :], wk[:, kc, :], start=(kc == 0), stop=(kc == 15))
                            for kc in range(16):
                                P.mm(pb[1][:], hT[mb][:, kc, :], wv[:, kc, :], start=(kc == 0), stop=(kc == 15))
                            P.copy("act", mkf[:], pb[0][:])
                            P.dma("sp", o_mk[mb * 128:(mb + 1) * 128, :], mkf[:])
                            P.copy("dve", mkb[:], pb[0][:])
                            for h in range(4):
                                P.tr(ptm[:, h * 128:(h + 1) * 128], mkb[:, h * 128:(h + 1) * 128], idb[:])
                            P.copy("dve", MKT[:, 0, :, mb * 128:(mb + 1) * 128], ptm[:, 0:512].rearrange("p (h m) -> p h m", h=4))
                            P.copy("act", mkf[:], pb[1][:])
                            P.dma("sp", o_mv[mb * 128:(mb + 1) * 128, :], mkf[:])
                            P.copy("dve", MVa[:, 0, mb, :, 0:128], pb[1][:].rearrange("p (h d) -> p h d", h=4))
                    for bb in range(4):
                        for mb in range(2):
                            P.dma("sp", mkf[:], cmk_d[bb, mb * 128:(mb + 1) * 128, :])
                            P.copy("dve", mkb[:], mkf[:])
                            for h in range(4):
                                P.tr(ptm[:, h * 128:(h + 1) * 128], mkb[:, h * 128:(h + 1) * 128], idb[:])
                            P.copy("dve", MKT[:, 1 + bb, :, mb * 128:(mb + 1) * 128], ptm[:, 0:512].rearrange("p (h m) -> p h m", h=4))
                            P.dma("pool", MVa[:, 1 + bb, mb, :, 0:128], cmv_d[bb, mb * 128:(mb + 1) * 128, :].rearrange("m (h d) -> m h d", h=4))

                    load_grow(gA, 2)
                    load_grow(gB, 3)
                    gC = P.sb("gC", [128, D], F32)
                    load_grow(gC, 4)
                    qT = P.sb("qT", [128, 4, 128], BF16)
                    pxT = P.sb("pxT", [128, 8, 128], BF16)
                    pxs = P.sb("pxs", [128, 4, 4, 2, 32], BF16)
                    P.memset("pool", pxs[:].rearrange("p a b c d -> p (a b c d)"), 0.0)
                    ox = P.sb("ox", [128, 512], BF16)
                    oT = P.sb("oT", [128, 4, 128], BF16)
                    psx = [pb[0], pb[1]]
                    pox = [pb[2], pb[3]]
                    for blk in range(9):
                        i2 = blk % 2
                        x1 = xa[i2]
                        P.dma("sp", x1[:], xr_d[blk * 128:(blk + 1) * 128, :])
                        P.memset("dve", sb2[:, 0:8], 0.0)
                        P.act(jb[:], x1[:], AF.Square, accum_out=sb2[:, 0:1])
                        rstd_of(sb2[:, 1:2], sb2[:, 0:1], D)
                        P.stt("dve", xsb[i2][:], x1[:], sb2[:, 1:2], gA[:], ALU.mult, ALU.mult)
                        for kc in range(16):
                            P.tr(ptr[kc // 8][:, (kc % 8) * 128:(kc % 8 + 1) * 128], xsb[i2][:, kc * 128:(kc + 1) * 128], idb[:])
                        P.copy("dve", hT[i2][:, 0:8, :].rearrange("p a b -> p (a b)"), ptr[0][:])
                        P.copy("act", hT[i2][:, 8:16, :].rearrange("p a b -> p (a b)"), ptr[1][:])
                        for h in range(4):
                            for kc in range(16):
                                P.mm(pqx[:, h * 128:(h + 1) * 128], wq[:, kc, h * 128:(h + 1) * 128], hT[i2][:, kc, :], start=(kc == 0), stop=(kc == 15))
                        P.copy("act", qT[:].rearrange("p a b -> p (a b)"), pqx[:])
                        xsc = 128.0 ** -0.5
                        if blk < 8:
                            for h in range(4):
                                for mb in range(2):
                                    e = h * 2 + mb
                                    P.mm(psx[e // 4][:, (e % 4) * 128:(e % 4 + 1) * 128], MKT[:, 0, h, mb * 128:(mb + 1) * 128], qT[:, h, :])
                            P.act(pxT[:, 0:4, :].rearrange("p a b -> p (a b)"), psx[0][:], AF.Exp, scale=xsc)
                            P.act(pxT[:, 4:8, :].rearrange("p a b -> p (a b)"), psx[1][:], AF.Exp, scale=xsc)
                            for h in range(4):
                                for mb in range(2):
                                    P.mm(pox[h // 2][:, (h % 2) * 129:(h % 2) * 129 + 129], pxT[:, h * 2 + mb, :], MVa[:, 0, mb, h, :],
                                         start=(mb == 0), stop=(mb == 1))
                            for h in range(4):
                                oo = pox[h // 2][:, (h % 2) * 129:(h % 2) * 129 + 129]
                                P.recip(sb2[:, 8 + h:9 + h], oo[:, 128:129])
                                P.tss("dve", ox[:, h * 128:(h + 1) * 128], oo[:, 0:128], sb2[:, 8 + h:9 + h], ALU.mult)
                        else:
                            for bb in range(4):
                                for h in range(4):
                                    for mb in range(2):
                                        e = (bb * 4 + h) * 2 + mb
                                        P.mm(psx[0][:, e * 8:e * 8 + 8], MKT[:, 1 + bb, h, mb * 128:(mb + 1) * 128], qT[:, h, bb * 8:bb * 8 + 8])
                            for bb in range(4):
                                P.act(pxs[:, bb, :, :, bb * 8:bb * 8 + 8],
                                      psx[0][:, bb * 64:(bb + 1) * 64].rearrange("p (h m q) -> p h m q", h=4, m=2), AF.Exp, scale=xsc)
                            P.memset("dve", ox[:], 0.0)
                            for h in range(4):
                                k = 0
                                for bb in range(4):
                                    for mb in range(2):
                                        P.mm(pox[h // 2][0:32, (h % 2) * 129:(h % 2) * 129 + 129], pxs[:, bb, h, mb, :], MVa[:, 1 + bb, mb, h, :],
                                             start=(k == 0), stop=(k == 7))
                                        k += 1
                            for h in range(4):
                                oo = pox[h // 2][0:32, (h % 2) * 129:(h % 2) * 129 + 129]
                                P.recip(sb2[0:32, 8 + h:9 + h], oo[:, 128:129])
                                P.tss("dve", ox[0:32, h * 128:(h + 1) * 128], oo[:, 0:128], sb2[0:32, 8 + h:9 + h], ALU.mult)
                        for h in range(4):
                            P.tr(ptm[:, h * 128:(h + 1) * 128], ox[:, h * 128:(h + 1) * 128], idb[:])
                        P.copy("dve", oT[:].rearrange("p a b -> p (a b)"), ptm[:, 0:512])
                        P.memset("dve", sb2[:, 2:6], 0.0)
                        for ng in range(4):
                            for h in range(4):
                                P.mm(pb[ng][:], oT[:, h, :], wxo[:, h, ng * 512:(ng + 1) * 512], start=(h == 0), stop=(h == 3))
                            P.act(jb[:, 0:512], pb[ng][:], AF.Square, accum_out=sb2[:, 2 + ng:3 + ng])
                        P.rsum("dve", sb2[:, 6:7], sb2[:, 2:6])
                        rstd_of(sb2[:, 7:8], sb2[:, 6:7], D)
                        for ng in range(4):
                            sl = slice(ng * 512, (ng + 1) * 512)
                            P.stt("dve", mkf[:], pb[ng][:], sb2[:, 7:8], gB[:, sl], ALU.mult, ALU.mult)
                            P.tt("pool", x1[:, sl], x1[:, sl], mkf[:], ALU.add)
                        P.dma("sp", xr_d[blk * 128:(blk + 1) * 128, :], x1[:])
                        P.memset("dve", sb2[:, 0:1], 0.0)
                        P.act(jb[:], x1[:], AF.Square, accum_out=sb2[:, 0:1])
                        rstd_of(sb2[:, 1:2], sb2[:, 0:1], D)
                        P.stt("dve", xsb[i2][:], x1[:], sb2[:, 1:2], gC[:], ALU.mult, ALU.mult)
                        for kc in range(16):
                            P.tr(ptr[kc // 8][:, (kc % 8) * 128:(kc % 8 + 1) * 128], xsb[i2][:, kc * 128:(kc + 1) * 128], idb[:])
                        P.copy("dve", h3T[:, 0:8, blk * 128:(blk + 1) * 128], ptr[0][:].rearrange("p (a b) -> p a b", a=8))
                        P.copy("act", h3T[:, 8:16, blk * 128:(blk + 1) * 128], ptr[1][:].rearrange("p (a b) -> p a b", a=8))

                if kstop == "B1b":
                    dbg_dump()
                    dbg_xr()
                    raise _Stop()
                actT = P.sb("actT", [128, NFF, TB], BF16)
                with P.phase():
                    wg = [P.sb("wg%d" % i, [128, 16, 128], BF16) for i in range(2)]
                    wu = [P.sb("wu%d" % i, [128, 16, 128], BF16) for i in range(2)]
                    sg = [P.sb("sg%d" % i, [128, 512], F32) for i in range(2)]
                    pg = [P.ps("f_pg%d" % i, [128, 512], F32) for i in range(2)]
                    pu2 = [P.ps("f_pu%d" % i, [128, 512], F32) for i in range(2)]
                    NG = [(0, 512), (512, 1024), (1024, 1152)]
                    u = 0
                    for f in range(NFF):
                        w2 = f % 2
                        P.dma("pool", wg[w2][:].rearrange("p a b -> p (a b)"), w_gate[f])
                        P.dma("pool", wu[w2][:].rearrange("p a b -> p (a b)"), w_up[f])
                        for (n0, n1) in NG:
                            bi = u % 2
                            u += 1
                            n = n1 - n0
                            for kc in range(16):
                                P.mm(pg[bi][:, 0:n], wg[w2][:, kc, :], h3T[:, kc, n0:n1], start=(kc == 0), stop=(kc == 15))
                            for kc in range(16):
                                P.mm(pu2[bi][:, 0:n], wu[w2][:, kc, :], h3T[:, kc, n0:n1], start=(kc == 0), stop=(kc == 15))
                            P.act(sg[bi][:, 0:n], pg[bi][:, 0:n], AF.Silu)
                            P.tt("dve", actT[:, f, n0:n1], sg[bi][:, 0:n], pu2[bi][:, 0:n], ALU.mult)

            with P.phase():
                fT_d = P.dram("fT_d", [16, 128, TB], BF16)
                fTs = [P.sb("fTs%d" % i, [128, 512], BF16) for i in range(2)]
                fTb = [P.sb("fTb%d" % i, [128, 16, 128], BF16) for i in range(2)]
                wd = [P.sb("wd%d" % i, [128, NFF, 128], BF16) for i in range(2)]
                pd = [P.ps("d_pd%d" % i, [128, 512], F32) for i in range(4)]
                NG = [(0, 512), (512, 1024), (1024, 1152)]
                u = 0
                for dc in range(16):
                    w2 = dc % 2
                    P.dma("pool", wd[w2][:].rearrange("p a b -> p (a b)"), w_down[dc])
                    for (n0, n1) in NG:
                        bi = u % 4
                        u += 1
                        n = n1 - n0
                        for f in range(NFF):
                            P.mm(pd[bi][:, 0:n], wd[w2][:, f, :], actT[:, f, n0:n1], start=(f == 0), stop=(f == NFF - 1))
                        P.copy("act" if u % 2 else "dve", fTs[u % 2][:, 0:n], pd[bi][:, 0:n])
                        P.dma("sp", fT_d[dc, :, n0:n1], fTs[u % 2][:, 0:n])
                gD = P.sb("gD", [128, D], F32)
                load_grow(gD, 5)
                ptr = [P.ps("e_ptr%d" % i, [128, 1024], BF16) for i in range(2)]
                xe = [P.sb("xe%d" % i, [128, D], F32) for i in range(1)] * 2
                ye = [P.sb("ye%d" % i, [128, D], F32) for i in range(1)] * 2
                je = P.sb("je", [128, 1024], BF16)
                se = P.sb("se", [128, 8], F32)
                for blk in range(9):
                    i2 = blk % 2
                    P.dma("sp", xe[i2][:], xr_d[blk * 128:(blk + 1) * 128, :])
                    P.dma("sp", fTb[i2][:], fT_d[:, :, blk * 128:(blk + 1) * 128].rearrange("c p t -> p c t"))
                    P.memset("dve", se[:, 0:2], 0.0)
                    for kc in range(16):
                        P.tr(ptr[kc // 8][:, (kc % 8) * 128:(kc % 8 + 1) * 128], fTb[i2][:, kc, :], idb[:])
                    for hh in range(2):
                        P.act(je[:], ptr[hh][:], AF.Square, accum_out=se[:, hh:hh + 1])
                    P.rsum("dve", se[:, 2:3], se[:, 0:2])
                    rstd_of(se[:, 3:4], se[:, 2:3], D)
                    for hh in range(2):
                        sl = slice(hh * 1024, (hh + 1) * 1024)
                        P.stt("dve", ye[i2][:, sl], ptr[hh][:], se[:, 3:4], gD[:, sl], ALU.mult, ALU.mult)
                    P.tt("pool", ye[i2][:], ye[i2][:], xe[i2][:], ALU.add)
                    P.dma("sp", o_y[blk * 128:(blk + 1) * 128, :], ye[i2][:])

        if kstop is not None:
            dbg_dump()
            dbg_xr()
        P.final_wait([t.ap() for t in (o_k, o_v, o_retp, o_rets, o_y, o_mk, o_mv) if t is not None])
        P.emit()
    return nc

_NC_CACHE = {}


def _wl(w, ncol_chunks=None):
    K, N = w.shape
    return np.ascontiguousarray(w.reshape(K // 128, 128, N).transpose(1, 0, 2)).reshape(128, (K // 128) * N)


def kernel(x_prompt, x_sample, cache_k, cache_v, state_ret, cache_mem_k, cache_mem_v, page_table, mem_prompt,
           rel_bias, norm_pre_mix, norm_post_mix, norm_pre_x, norm_post_x, norm_pre_ffn, norm_post_ffn, norm_mem,
           w_in, w_out, ret_gn, diff_gn, lam_q1, lam_k1, lam_q2, lam_k2,
           w_xq, w_xk, w_xv, w_xo, w_gate, w_up, w_down):
    FUSED = _NC_CACHE.get("fused", True)
    in_maps = _prep(x_prompt, x_sample, cache_k, cache_v, state_ret, cache_mem_k, cache_mem_v, page_table, mem_prompt,
                    rel_bias, norm_pre_mix, norm_post_mix, norm_pre_x, norm_post_x, norm_pre_ffn, norm_post_ffn, norm_mem,
                    w_in, w_out, ret_gn, diff_gn, lam_q1, lam_k1, lam_q2, lam_k2,
                    w_xq, w_xk, w_xv, w_xo, w_gate, w_up, w_down)
    if FUSED:
        if "nc" not in _NC_CACHE:
            _NC_CACHE["nc"] = build_program()
        res = run_bass_kernel_spmd(_NC_CACHE["nc"], in_maps, core_ids=list(range(NCORE)))
        return _post(res.results)
    if "ncA" not in _NC_CACHE:
        _NC_CACHE["ncA"] = build_program(part="A")
        _NC_CACHE["ncB"] = build_program(part="B")
    AK = ("x_all", "w_in", "grows", "tabs", "hc", "rb", "lamv", "gn", "state", "ck", "cv", "pt")
    BK = ("grows", "x_own", "gidx", "mem_own", "cmk", "cmv", "w_out", "w_xq", "w_xk", "w_xv", "w_xo", "w_gate", "w_up", "w_down")
    resA = run_bass_kernel_spmd(_NC_CACHE["ncA"], [{k: m[k] for k in AK} for m in in_maps], core_ids=list(range(NCORE)))
    RA = resA.results
    ag_all = np.concatenate([RA[c]["ag"] for c in range(NCORE)], axis=0)
    resB = run_bass_kernel_spmd(_NC_CACHE["ncB"], [dict({k: m[k] for k in BK}, ag_all=ag_all) for m in in_maps], core_ids=list(range(NCORE)))
    RB = resB.results
    return _post([dict(RA[c], **RB[c]) for c in range(NCORE)])


def _prep(x_prompt, x_sample, cache_k, cache_v, state_ret, cache_mem_k, cache_mem_v, page_table, mem_prompt,
          rel_bias, norm_pre_mix, norm_post_mix, norm_pre_x, norm_post_x, norm_pre_ffn, norm_post_ffn, norm_mem,
          w_in, w_out, ret_gn, diff_gn, lam_q1, lam_k1, lam_q2, lam_k2,
          w_xq, w_xk, w_xv, w_xo, w_gate, w_up, w_down):
    f32 = np.float32
    A = lambda a: np.asarray(a)
    x_prompt, x_sample = A(x_prompt), A(x_sample)

    x_all = np.concatenate([x_prompt.reshape(8192, D), x_sample.reshape(256, D)], axis=0)
    win = A(w_in)[0]
    wout = A(w_out)[0]
    grows = np.stack([np.broadcast_to(A(g)[0][None, :], (128, D)) for g in
                      (norm_pre_mix, norm_post_mix, norm_pre_x, norm_post_x, norm_pre_ffn, norm_post_ffn, norm_mem)]).astype(f32)
    lamv = np.broadcast_to(np.concatenate([A(lam_q1)[0], A(lam_k1)[0], A(lam_q2)[0], A(lam_k2)[0]])[None, :], (128, 256)).astype(f32)
    pt = A(page_table).astype(np.int32)
    pt_pairs = np.ascontiguousarray(pt.reshape(16, 2, 64).transpose(1, 2, 0).reshape(128, 16))
    ck = A(cache_k)[0]
    cv = A(cache_v)[0]
    st = A(state_ret)[0]
    worder = np.concatenate([np.concatenate([np.arange(h * 128, (h + 1) * 128), 1024 + np.arange(h * 128, (h + 1) * 128)]) for h in range(8)])
    wout_l = _wl(wout[worder])
    wxq_l, wxk_l, wxv_l = _wl(A(w_xq)[0]), _wl(A(w_xk)[0]), _wl(A(w_xv)[0])
    wxo_l = _wl(A(w_xo)[0])
    wg = A(w_gate)[0]
    wu = A(w_up)[0]
    wd = A(w_down)[0]
    wg_l = np.ascontiguousarray(wg.reshape(16, 128, NFF, 128).transpose(2, 1, 0, 3)).reshape(NFF, 128, 16 * 128)
    wu_l = np.ascontiguousarray(wu.reshape(16, 128, NFF, 128).transpose(2, 1, 0, 3)).reshape(NFF, 128, 16 * 128)
    wd_l = np.ascontiguousarray(wd.reshape(NFF, 128, 16, 128).transpose(2, 1, 0, 3)).reshape(16, 128, NFF * 128)
    cmk = A(cache_mem_k)[0].reshape(32, 256, 512)
    cmv = A(cache_mem_v)[0].reshape(32, 256, 512)
    memp = A(mem_prompt)
    xs_flat = x_sample.reshape(256, D)
    xp_flat = x_prompt.reshape(8192, D)

    in_maps = []
    for c in range(NCORE):
        cols = np.concatenate([np.arange(seg * 1024 + c * 128, seg * 1024 + (c + 1) * 128) for seg in range(7)])
        tabs, hc = _head_tables(c)
        x_own = np.zeros((TB, D), f32)
        x_own[:1024] = xp_flat[c * 1024:(c + 1) * 1024]
        x_own[1024:1056] = xs_flat[c * 32:(c + 1) * 32]
        tok = np.zeros(TB, np.int64)
        tok[:1024] = c * 1024 + np.arange(1024)
        tok[1024:1056] = 8192 + c * 32 + np.arange(32)
        gidx = np.zeros((128, 72), np.int32)
        for blk in range(9):
            for h in range(8):
                gidx[:, blk * 8 + h] = h * NTOK + tok[blk * 128:(blk + 1) * 128]
        m = {
            "x_all": x_all,
            "w_in": _wl(win[:, cols]),
            "grows": grows,
            "tabs": tabs,
            "hc": hc,
            "rb": np.broadcast_to(A(rel_bias)[:, c][None, :], (128, 32)).astype(f32),
            "lamv": lamv,
            "gn": np.broadcast_to(np.concatenate([A(ret_gn)[0][c * 128:(c + 1) * 128], A(diff_gn)[0][c * 128:(c + 1) * 128]])[None, :], (128, 256)).astype(f32),
            "state": np.ascontiguousarray(st[:, c].transpose(1, 0, 2)).reshape(128, 32 * 128),
            "ck": np.ascontiguousarray(ck[:, :, c, :]).reshape(-1, 16384),
            "cv": np.ascontiguousarray(cv[:, :, c, :]).reshape(-1, 16384),
            "pt": pt_pairs,
            "x_own": x_own,
            "gidx": gidx,
            "mem_own": np.ascontiguousarray(memp[c // 4]),
            "cmk": np.ascontiguousarray(cmk[4 * c:4 * c + 4]),
            "cmv": np.ascontiguousarray(cmv[4 * c:4 * c + 4]),
            "w_out": wout_l, "w_xq": wxq_l, "w_xk": wxk_l, "w_xv": wxv_l, "w_xo": wxo_l,
            "w_gate": wg_l, "w_up": wu_l, "w_down": wd_l,
        }
        in_maps.append(m)

    return in_maps


def _post(R):
    y_prompt = np.concatenate([R[c]["o_y"][:1024] for c in range(NCORE)], axis=0).reshape(2, 4096, D)
    y_sample = np.concatenate([R[c]["o_y"][1024:1056] for c in range(NCORE)], axis=0).reshape(32, 8, D)
    k_all = np.stack([R[c]["o_k"] for c in range(NCORE)], axis=1)
    v_all = np.stack([R[c]["o_v"] for c in range(NCORE)], axis=1)
    k_prompt = k_all[:8192].reshape(1, 2, 4096, 8, 128)
    v_prompt = v_all[:8192].reshape(1, 2, 4096, 8, 128)
    k_sample = k_all[8192:].reshape(1, 32, 8, 8, 128)
    v_sample = v_all[8192:].reshape(1, 32, 8, 8, 128)
    ret_prompt = np.stack([R[c]["o_retp"] for c in range(NCORE)], axis=1).reshape(1, 2, 8, 128, 128)
    ret_sample = np.stack([R[c]["o_rets"] for c in range(NCORE)], axis=1).reshape(1, 32, 8, 128, 128)
    mem_k_prompt = np.stack([R[0]["o_mk"], R[4]["o_mk"]]).reshape(1, 2, 256, 4, 128)
    mem_v_prompt = np.stack([R[0]["o_mv"], R[4]["o_mv"]]).reshape(1, 2, 256, 4, 128)
    outs = (y_prompt, y_sample, k_prompt, v_prompt, ret_prompt, mem_k_prompt, mem_v_prompt, k_sample, v_sample, ret_sample)
    return tuple(np.ascontiguousarray(o, dtype=np.float32) for o in outs)
```
